# Optimizing a Trainium2 kernel written in Bass

```python
import jax
import jax.numpy as jnp
from jax import lax
import numpy as np

D_MODEL = 2048
BATCH = 2
SEQ = 8192
DEPTH = 4

HEAD_DIM = 128
N_Q_HEADS = D_MODEL // HEAD_DIM
N_KV_HEADS = N_Q_HEADS // 4
Q_PER_KV = N_Q_HEADS // N_KV_HEADS
ATTN_WIDTH = N_Q_HEADS * HEAD_DIM
KV_WIDTH = N_KV_HEADS * HEAD_DIM
WINDOW = 128
BLOCK = 128
ROPE_THETA = 500000.0
ROT_DIM = HEAD_DIM // 4
CONV_WIDTH = D_MODEL // 2
CONV_SIZE = 31
CONV_PAD = CONV_SIZE // 2
POOL_WIDTH = D_MODEL // 2
POOL_SIZES = (2, 4, 8, 16)
N_POOL_GROUPS = len(POOL_SIZES)
POOL_GROUP = POOL_WIDTH // N_POOL_GROUPS
N_BRANCHES = 3
D_FF = ((8 * D_MODEL // 3 + 255) // 256) * 256
RMS_EPS = 1e-6
LN_EPS = 1e-5
NEG_INF = -1e30
IN_SIZES = (ATTN_WIDTH, KV_WIDTH, KV_WIDTH, CONV_WIDTH, CONV_WIDTH, POOL_WIDTH, N_BRANCHES * D_MODEL)
IN_SPLITS = [int(s) for s in np.cumsum(IN_SIZES)[:-1]]
N_IN = sum(IN_SIZES)

kernel_name = 'hybrid_gated_encoder_trunk'


def rmsnorm(x, g):
    xf = x.astype(jnp.float32)
    y = xf * lax.rsqrt(jnp.mean(xf * xf, axis=-1, keepdims=True) + RMS_EPS)
    return (y * g.astype(jnp.float32)).astype(x.dtype)


def layernorm(x, g, b):
    xf = x.astype(jnp.float32)
    mu = jnp.mean(xf, axis=-1, keepdims=True)
    var = jnp.mean(jnp.square(xf - mu), axis=-1, keepdims=True)
    y = (xf - mu) * lax.rsqrt(var + LN_EPS)
    return (y * g.astype(jnp.float32) + b.astype(jnp.float32)).astype(x.dtype)


def swiglu(x, w_gate, w_up, w_down):
    return (jax.nn.silu(x @ w_gate) * (x @ w_up)) @ w_down


def rope_tables(seq, dtype):
    pos = jnp.arange(seq, dtype=jnp.float32)
    inv_freq = ROPE_THETA ** (-jnp.arange(0, ROT_DIM, 2, dtype=jnp.float32) / ROT_DIM)
    ang = pos[:, None] * inv_freq[None, :]
    return jnp.cos(ang)[:, None, :].astype(dtype), jnp.sin(ang)[:, None, :].astype(dtype)


def partial_rotary(x, cos, sin):
    half = ROT_DIM // 2
    x1 = x[..., :half]
    x2 = x[..., half:ROT_DIM]
    return jnp.concatenate([x1 * cos - x2 * sin, x2 * cos + x1 * sin, x[..., ROT_DIM:]], axis=-1)


def windowed_gqa(q, k, v, sink):
    B, S = q.shape[0], q.shape[1]
    nb = S // BLOCK
    pad = ((0, 0), (BLOCK, BLOCK), (0, 0), (0, 0))
    kp = jnp.pad(k, pad).reshape(B, nb + 2, BLOCK, N_KV_HEADS, HEAD_DIM)
    vp = jnp.pad(v, pad).reshape(B, nb + 2, BLOCK, N_KV_HEADS, HEAD_DIM)
    kb = jnp.concatenate([kp[:, :nb], kp[:, 1:nb + 1], kp[:, 2:]], axis=2)
    vb = jnp.concatenate([vp[:, :nb], vp[:, 1:nb + 1], vp[:, 2:]], axis=2)
    qb = q.reshape(B, nb, BLOCK, N_KV_HEADS, Q_PER_KV, HEAD_DIM)
    logits = jnp.einsum('bnqhgd,bnkhd->bnhgqk', qb, kb).astype(jnp.float32) * (HEAD_DIM ** -0.5)
    qi = jnp.arange(BLOCK)[:, None]
    kj = jnp.arange(3 * BLOCK)[None, :]
    in_window = jnp.abs(kj - BLOCK - qi) <= WINDOW
    key_pos = jnp.arange(nb)[:, None] * BLOCK - BLOCK + jnp.arange(3 * BLOCK)[None, :]
    in_seq = (key_pos >= 0) & (key_pos < S)
    mask = in_window[None, :, :] & in_seq[:, None, :]
    logits = jnp.where(mask[None, :, None, None], logits, NEG_INF)
    s = sink.astype(jnp.float32).reshape(N_KV_HEADS, Q_PER_KV)[None, None, :, :, None, None]
    m = jnp.maximum(jnp.max(logits, axis=-1, keepdims=True), s)
    p = jnp.exp(logits - m)
    probs = p / (jnp.sum(p, axis=-1, keepdims=True) + jnp.exp(s - m))
    out = jnp.einsum('bnhgqk,bnkhd->bnqhgd', probs.astype(v.dtype), vb)
    return out.reshape(B, S, ATTN_WIDTH)


def conformer_conv(a, g, w_dw, b_dw, ln_g, ln_b, w_proj):
    y = a * jax.nn.sigmoid(g)
    y = lax.conv_general_dilated(y, w_dw[:, None, :], window_strides=(1,),
                                 padding=[(CONV_PAD, CONV_PAD)],
                                 dimension_numbers=('NWC', 'WIO', 'NWC'),
                                 feature_group_count=CONV_WIDTH) + b_dw
    y = jax.nn.silu(layernorm(y, ln_g, ln_b))
    return y @ w_proj


def multiscale_pool(p, pool_w, pool_scale, w_proj):
    B, S, _ = p.shape
    pg = p.reshape(B, S, N_POOL_GROUPS, POOL_GROUP).astype(jnp.float32)
    cs = jnp.pad(jnp.cumsum(pg, axis=1), ((0, 0), (1, 0), (0, 0), (0, 0)))
    pos = jnp.arange(S)[:, None]
    half = jnp.array(POOL_SIZES, dtype=jnp.int32)[None, :] // 2
    lo = jnp.clip(pos - half, 0, S - 1)
    hi = jnp.clip(pos + half - 1, 0, S - 1)
    grp = jnp.arange(N_POOL_GROUPS)[None, :]
    total = cs[:, hi + 1, grp, :] - cs[:, lo, grp, :]
    mean = total / (hi - lo + 1).astype(jnp.float32)[None, :, :, None]
    mixed = (mean - pg).astype(p.dtype)
    y = jnp.einsum('bsgc,gce->bsge', mixed, pool_w).reshape(B, S, POOL_WIDTH) * pool_scale
    return y @ w_proj


def hybrid_mixer(u, w_in, sink, w_attn_proj, conv_dw, conv_dw_b, conv_ln_g, conv_ln_b, w_conv_proj,
                 pool_w, pool_scale, w_pool_proj, w_out, cos, sin):
    B, S, _ = u.shape
    z = u @ w_in
    q, k, v, conv_a, conv_g, pool_in, gate_in = jnp.split(z, IN_SPLITS, axis=-1)
    q = partial_rotary(q.reshape(B, S, N_Q_HEADS, HEAD_DIM), cos, sin)
    k = partial_rotary(k.reshape(B, S, N_KV_HEADS, HEAD_DIM), cos, sin)
    v = v.reshape(B, S, N_KV_HEADS, HEAD_DIM)
    attn = windowed_gqa(q, k, v, sink) @ w_attn_proj
    conv = conformer_conv(conv_a, conv_g, conv_dw, conv_dw_b, conv_ln_g, conv_ln_b, w_conv_proj)
    pool = multiscale_pool(pool_in, pool_w, pool_scale, w_pool_proj)
    gates = jax.nn.sigmoid(gate_in.reshape(B, S, N_BRANCHES, D_MODEL))
    merged = gates[:, :, 0] * attn + gates[:, :, 1] * conv + gates[:, :, 2] * pool
    return merged @ w_out


def setup_inputs(seed: int = 0) -> dict:
    key = jax.random.key(seed)
    ks = iter(jax.random.split(key, 32))
    L, D = DEPTH, D_MODEL

    def dense(shape, fan_in):
        return jax.random.normal(next(ks), shape, jnp.float32) * (fan_in ** -0.5)

    def gain(shape):
        return 1.0 + 0.02 * jax.random.normal(next(ks), shape, jnp.float32)

    def small(shape, scale):
        return scale * jax.random.normal(next(ks), shape, jnp.float32)

    return {
        'x': jax.random.normal(next(ks), (BATCH, SEQ, D), jnp.float32),
        'ffn1_pre_g': gain((L, D)),
        'ffn1_post_g': gain((L, D)),
        'ffn1_w_gate': dense((L, D, D_FF), D),
        'ffn1_w_up': dense((L, D, D_FF), D),
        'ffn1_w_down': dense((L, D_FF, D), D_FF),
        'mix_pre_g': gain((L, D)),
        'mix_post_g': gain((L, D)),
        'w_in': dense((L, D, N_IN), D),
        'sink': small((L, N_Q_HEADS), 0.5),
        'w_attn_proj': dense((L, ATTN_WIDTH, D), ATTN_WIDTH),
        'conv_dw': dense((L, CONV_SIZE, CONV_WIDTH), CONV_SIZE),
        'conv_dw_b': small((L, CONV_WIDTH), 0.02),
        'conv_ln_g': gain((L, CONV_WIDTH)),
        'conv_ln_b': small((L, CONV_WIDTH), 0.02),
        'w_conv_proj': dense((L, CONV_WIDTH, D), CONV_WIDTH),
        'pool_w': dense((L, N_POOL_GROUPS, POOL_GROUP, POOL_GROUP), POOL_GROUP),
        'pool_scale': gain((L, POOL_WIDTH)),
        'w_pool_proj': dense((L, POOL_WIDTH, D), POOL_WIDTH),
        'w_out': dense((L, D, D), D),
        'ffn2_pre_g': gain((L, D)),
        'ffn2_post_g': gain((L, D)),
        'ffn2_w_gate': dense((L, D, D_FF), D),
        'ffn2_w_up': dense((L, D, D_FF), D),
        'ffn2_w_down': dense((L, D_FF, D), D_FF),
    }


def reference(x, ffn1_pre_g, ffn1_post_g, ffn1_w_gate, ffn1_w_up, ffn1_w_down,
              mix_pre_g, mix_post_g, w_in, sink, w_attn_proj, conv_dw, conv_dw_b, conv_ln_g, conv_ln_b,
              w_conv_proj, pool_w, pool_scale, w_pool_proj, w_out,
              ffn2_pre_g, ffn2_post_g, ffn2_w_gate, ffn2_w_up, ffn2_w_down):
    cos, sin = rope_tables(x.shape[1], x.dtype)
    h = x
    for l in range(DEPTH):
        f1 = swiglu(rmsnorm(h, ffn1_pre_g[l]), ffn1_w_gate[l], ffn1_w_up[l], ffn1_w_down[l])
        h = h + 0.5 * rmsnorm(f1, ffn1_post_g[l])
        u = rmsnorm(h, mix_pre_g[l])
        mix = hybrid_mixer(u, w_in[l], sink[l], w_attn_proj[l], conv_dw[l], conv_dw_b[l], conv_ln_g[l],
                           conv_ln_b[l], w_conv_proj[l], pool_w[l], pool_scale[l], w_pool_proj[l], w_out[l],
                           cos, sin)
        h = h + rmsnorm(mix, mix_post_g[l])
        f2 = swiglu(rmsnorm(h, ffn2_pre_g[l]), ffn2_w_gate[l], ffn2_w_up[l], ffn2_w_down[l])
        h = h + 0.5 * rmsnorm(f2, ffn2_post_g[l])
    return h
```

```python
import numpy as np
import ml_dtypes
from contextlib import ExitStack
import concourse.bass as bass
import concourse.mybir as mybir
from concourse.bass_utils import run_bass_kernel_spmd

F32 = mybir.dt.float32
BF16 = mybir.dt.bfloat16
AF = mybir.ActivationFunctionType
ALU = mybir.AluOpType
AX = mybir.AxisListType

D = 2048
DFF = 5632
NIN = 12288
KC = 16
FC = 44
SEQ = 8192
BATCH = 2
DEPTH = 4
NCORES = 2
OWN = 8192
CPS = SEQ // OWN
OWNB = OWN // 128
HD = 128
RMS_EPS = 1e-6
LN_EPS = 1e-5
ROPE_THETA = 500000.0
SCALE = HD ** -0.5
NEG = -1e30

V_F1PRE, V_F1POST, V_MPRE, V_MPOST, V_F2PRE, V_F2POST = 0, 16, 32, 48, 64, 80
V_CB, V_LNG, V_LNB, V_PSC, V_CDW, V_SINK = 96, 104, 112, 120, 128, 376
VL = 392

ENGS = ['pe', 'act', 'dve', 'pool', 'sp']


class Op:
    __slots__ = ('eng', 'fn', 'cw', 'dw', 'signal', 'ticket', 'dma', 'idx', 'phase')


class Sched:
    def __init__(self, nc, es):
        self.nc = nc
        self.es = es
        self.esem = {e: es.enter_context(nc.semaphore('se_' + e)) for e in ENGS}
        self.ecnt = {e: 0 for e in ENGS}
        self.dsem = {}
        self.dcnt = {}
        self.lastw = {}
        self.readers = {}
        self.ops = []
        self.waited = {}
        self.phase = 0
        self.n = 0

    def _dsem(self, key):
        if key not in self.dsem:
            self.dsem[key] = self.es.enter_context(self.nc.semaphore('sd_' + key))
            self.dcnt[key] = 0
        return self.dsem[key]

    def op(self, eng, fn, reads=(), writes=(), dma=None, sync_same=False):
        o = Op()
        o.eng = eng
        o.fn = fn
        o.dma = dma
        o.signal = False
        o.ticket = None
        o.idx = self.n
        o.phase = self.phase
        self.n += 1
        deps = []
        for r in reads:
            w = self.lastw.get(r)
            if w is not None:
                deps.append(w)
        for r in writes:
            w = self.lastw.get(r)
            if w is not None:
                deps.append(w)
            deps.extend(self.readers.get(r, ()))
        for r in reads:
            self.readers.setdefault(r, []).append(o)
        for r in writes:
            self.lastw[r] = o
            self.readers[r] = []
        o.cw = {}
        o.dw = {}
        for d in deps:
            if d is o:
                continue
            if d.dma is not None:
                c = self.dcnt[d.dma]
                if o.dw.get(d.dma, 0) < c:
                    o.dw[d.dma] = c
            else:
                if d.phase != self.phase:
                    continue
                if d.eng == eng and dma is None and not sync_same:
                    continue
                p = o.cw.get(d.eng)
                if p is None or d.idx > p.idx:
                    o.cw[d.eng] = d
                d.signal = True
        if dma is not None:
            self._dsem(dma)
            self.dcnt[dma] += 16
        self.ops.append(o)
        return o

    def emit(self, block):
        def drain(eng):
            self.final_waits(eng)
            return eng.nop()
        self.op('sp', drain)
        per = {e: [] for e in ENGS}
        for o in self.ops:
            per[o.eng].append(o)
        for e in ENGS:
            for o in per[e]:
                if o.dma is None and o.signal:
                    self.ecnt[e] += 1
                    o.ticket = self.ecnt[e]
        meth = {'pe': block.tensor, 'act': block.scalar, 'dve': block.vector,
                'pool': block.gpsimd, 'sp': block.sync}
        for e in ENGS:
            ops = per[e]
            if not ops:
                continue

            def body(eng, ops=ops, e=e):
                for o in ops:
                    for de, d in o.cw.items():
                        key = (e, 'E' + de)
                        if self.waited.get(key, 0) < d.ticket:
                            eng.wait_ge(self.esem[de], d.ticket)
                            self.waited[key] = d.ticket
                    for k, c in o.dw.items():
                        key = (e, 'D' + k)
                        if self.waited.get(key, 0) < c:
                            eng.wait_ge(self.dsem[k], c)
                            self.waited[key] = c
                    ins = o.fn(eng)
                    if o.dma is not None:
                        ins.then_inc(self.dsem[o.dma], 16)
                    elif o.signal:
                        ins.then_inc(self.esem[e], 1)
            meth[e](body)
        self.ops = []
        for r in list(self.lastw.keys()):
            if self.lastw[r].dma is None:
                del self.lastw[r]
        for r in list(self.readers.keys()):
            lst = [x for x in self.readers[r] if x.dma is not None]
            if lst:
                self.readers[r] = lst
            else:
                del self.readers[r]
        self.phase += 1

    def final_waits(self, eng, e='sp'):
        for k, c in self.dcnt.items():
            if c > 0:
                eng.wait_ge(self.dsem[k], c)


class Rot:
    def __init__(self, aps, name):
        self.aps = aps
        self.name = name
        self.i = 0

    def next(self):
        k = self.i % len(self.aps)
        self.i += 1
        return self.aps[k], (self.name, k), '%s%d' % (self.name, k)


def tiles_of(b0, b1, maxb):
    out = []
    b = b0
    while b < b1:
        e = min(b + maxb, b1)
        out.append((b, e))
        b = e
    return out


class Prog:
    def __init__(self, NL, stop_after=None, dbg=None):
        self.dbg = dbg or {}
        self.NL = NL
        self.NB = OWNB + 2 * NL
        self.NT = self.NB * 128
        self.stop_after = stop_after
        self.nc = bass.Bass("TRN2", target_bir_lowering=False)
        self.build()

    def dram_in(self, name, shape, dt=F32):
        return self.nc.dram_tensor(name, list(shape), dt, kind="ExternalInput").ap()

    def dram_scr(self, name, shape, dt):
        return self.nc.dram_tensor(name, list(shape), dt).ap()

    def vcol(self, l, off, c=0):
        k = l * VL + off + c
        return self.vecs[:, k:k + 1]

    def hres(self, buf, b0, b1, cs=range(KC)):
        return [('h', buf, b, c) for b in range(b0, b1) for c in cs]

    def build(self):
        nc = self.nc
        NL, NT = self.NL, self.NT
        self.xT = self.dram_in("xT", [D, NT])
        W = {}
        for nm, shp in [('ffn1_w_gate', [D, DFF]), ('ffn1_w_up', [D, DFF]), ('ffn1_w_down', [DFF, D]),
                        ('w_in', [D, NIN]), ('w_attn_proj', [D, D]), ('w_conv_proj', [1024, D]),
                        ('pool_w', [1024, 256]), ('w_pool_proj', [1024, D]), ('w_out', [D, D]),
                        ('ffn2_w_gate', [D, DFF]), ('ffn2_w_up', [D, DFF]), ('ffn2_w_down', [DFF, D])]:
            W[nm] = self.dram_in(nm, [NL] + shp)
            setattr(self, 'wb_' + nm, self.dram_scr('wb_' + nm, shp, BF16))
        self.W = W
        self.vecs_d = self.dram_in("vecs", [128, NL * VL])
        self.cf32_d = self.dram_in("cf32", [128, 160])
        self.cbf_d = self.dram_in("cbf", [128, 640], BF16)
        self.cos_d = self.dram_in("cosT", [32, NT])
        self.sin_d = self.dram_in("sinT", [32, NT])
        self.valid_d = self.dram_in("valid", [1, NT])
        self.rc_d = self.dram_in("rc", [4, NT])
        self.kb_d = self.dram_in("kbias", [1, NT], BF16)
        self.yT = nc.dram_tensor("yT", [D, OWN], F32, kind="ExternalOutput").ap()
        if self.stop_after == 'm1':
            self.dq = nc.dram_tensor("dq", [D, NT], BF16, kind="ExternalOutput").ap()
            self.dk = nc.dram_tensor("dk", [512, NT], BF16, kind="ExternalOutput").ap()
            self.dv = nc.dram_tensor("dv", [NT, 512], BF16, kind="ExternalOutput").ap()
        if self.dbg.get('dump2'):
            self.d2a = nc.dram_tensor("d2a", [128, 16 * 384], BF16, kind="ExternalOutput").ap()
            self.d2k = nc.dram_tensor("d2k", [128, 4 * 640], BF16, kind="ExternalOutput").ap()
            self.d2v = nc.dram_tensor("d2v", [128, 5 * 512], BF16, kind="ExternalOutput").ap()
            self.d2q = nc.dram_tensor("d2q", [128, 16 * 384], BF16, kind="ExternalOutput").ap()
        if self.dbg.get('dump_att'):
            self.da = nc.dram_tensor("da", [2, 128, 16 * 384], BF16, kind="ExternalOutput").ap()
            self.dp = nc.dram_tensor("dp", [128, 384], BF16, kind="ExternalOutput").ap()
            self.de = nc.dram_tensor("de", [128, 384], F32, kind="ExternalOutput").ap()
            self.ds = nc.dram_tensor("ds", [128, 8], F32, kind="ExternalOutput").ap()
            self.dsc = nc.dram_tensor("dsc", [128, 384], F32, kind="ExternalOutput").ap()
        self.hbuf = {'X': self.xT, 'A': self.dram_scr('hA', [D, NT], F32), 'B': self.dram_scr('hB', [D, NT], F32)}
        self.q_d = self.dram_scr('q_d', [D, NT], BF16)
        self.k_d = self.dram_scr('k_d', [512, NT], BF16)
        self.v_d = self.dram_scr('v_d', [NT, 512], BF16)
        self.y_d = self.dram_scr('y_d', [1024, NT], BF16)
        self.p_d = self.dram_scr('p_d', [1024, NT], F32)
        self.g_d = self.dram_scr('g_d', [3 * D, NT], F32)

        with ExitStack() as es:
            self.S = Sched(nc, es)
            S = self.S
            self.vecs = es.enter_context(nc.sbuf_tensor("vecs_sb", [128, NL * VL], F32))
            self.cf32 = es.enter_context(nc.sbuf_tensor("cf32_sb", [128, 160], F32))
            self.cbf = es.enter_context(nc.sbuf_tensor("cbf_sb", [128, 640], BF16))
            self.ps = [es.enter_context(nc.psum_tensor("ps%d" % i, [128, 512], F32)) for i in range(5)]
            self.pst = es.enter_context(nc.psum_tensor("pst", [128, 1024], BF16))
            self.ps += [None, es.enter_context(nc.psum_tensor("ps6", [128, 512], F32)),
                        es.enter_context(nc.psum_tensor("ps7", [128, 512], F32))]
            self.psrot = Rot([p for p in self.ps[:5]], 'ps')
            self.ones32 = self.cf32[:, 0:128]
            self.pswap = self.cf32[0:32, 128:160]
            self.ident = self.cbf[:, 0:128]
            self.winmask = self.cbf[:, 128:512]
            self.onesrow = self.cbf[0:32, 512:640]

            with nc.Block() as block:
                S.op('sp', lambda e: e.dma_start(out=self.vecs[:], in_=self.vecs_d[:, :]), writes=['vecs'], dma='c0')
                S.op('sp', lambda e: e.dma_start(out=self.cf32[:], in_=self.cf32_d[:, :]), writes=['cf32'], dma='c0')
                S.op('sp', lambda e: e.dma_start(out=self.cbf[:], in_=self.cbf_d[:, :]), writes=['cbf'], dma='c0')
                self.cast_group(0, 'ffn1')
                S.emit(block)

            cur = 'X'
            done = False
            for l in range(NL):
                f1r = (l, self.NB - l)
                mr = (l + 1, self.NB - l - 1)
                dst = 'A' if cur == 'X' else cur
                self.ffn_phase(l, 1, cur, dst, f1r, lambda l=l: self.cast_group(l, 'mix'))
                cur = dst
                if self.stop_after == 'ffn1':
                    done = True
                    break
                other = 'B' if cur == 'A' else 'A'
                first = True
                for g0 in range(mr[0], mr[1], 16):
                    g1 = min(g0 + 16, mr[1])
                    self.m1_phase(l, cur, (g0 - 1, g1 + 1), (g0, g1),
                                  (lambda l=l: self.cast_group(l, 'ffn2')) if first else (lambda: None))
                    first = False
                if self.stop_after == 'm1':
                    done = True
                    break
                self.m2_phase(l, cur, other, mr)
                cur = other
                if self.stop_after == 'mix':
                    done = True
                    break
                last = (l == NL - 1)
                self.ffn_phase(l, 2, cur, cur, mr,
                               (lambda l=l: self.cast_group(l + 1, 'ffn1')) if not last else (lambda: None),
                               to_out=last)
            with nc.Block() as block:
                if self.stop_after == 'm1':
                    for (o_, i_) in [(self.dq, self.q_d), (self.dk, self.k_d), (self.dv, self.v_d)]:
                        S.op('sp', lambda e, o_=o_, i_=i_: e.dma_start(out=o_[:, :], in_=i_[:, :]),
                             reads=[r for r in list(S.lastw.keys()) if isinstance(r, tuple) and r[0] in ('q_d', 'k_d', 'v_d')],
                             writes=[('dbgout', id(o_))], dma='fin')
                if done:
                    ob = self.NL
                    src = self.hbuf[cur]
                    for c in range(KC):
                        S.op('sp', lambda e, c=c: e.dma_start(out=self.yT[c * 128:(c + 1) * 128, :],
                                                              in_=src[c * 128:(c + 1) * 128, ob * 128:ob * 128 + OWN]),
                             reads=self.hres(cur, ob, ob + OWNB, [c]), writes=[('out', c)], dma='fin')
                def fin(e):
                    S.final_waits(e)
                    return e.nop()
                S.op('sp', fin)
                S.emit(block)

    def cast_group(self, l, grp):
        if l >= self.NL:
            return
        S = self.S
        names = {'ffn1': ['ffn1_w_gate', 'ffn1_w_up', 'ffn1_w_down'],
                 'ffn2': ['ffn2_w_gate', 'ffn2_w_up', 'ffn2_w_down'],
                 'mix': ['w_in', 'w_attn_proj', 'w_conv_proj', 'pool_w', 'w_pool_proj', 'w_out']}[grp]
        for nm in names:
            src = self.W[nm]
            dst = getattr(self, 'wb_' + nm)
            F = dst.shape[1]
            cw = F
            while cw > 2048:
                cw //= 2
            for c0 in range(0, F, cw):
                S.op('pool', lambda e, src=src, dst=dst, c0=c0, cw=cw, l=l:
                     e.dma_start(out=dst[:, c0:c0 + cw], in_=src[l, :, c0:c0 + cw]),
                     writes=[('wb', nm)], dma='cast_' + nm)

    def norm_rstd(self, ps_ap, N, rtmp, rstd, nfeat, eps):
        S = self.S
        psr = ps_ap[1]
        S.op('act', lambda e: e.activation(out=rtmp[0][:, :N], in_=ps_ap[0][:, :N], func=AF.Sqrt,
                                           bias=float(eps), scale=1.0 / nfeat),
             reads=[psr], writes=[rtmp[1]])
        S.op('dve', lambda e: e.reciprocal(out=rstd[0][:, :N], in_=rtmp[0][:, :N]),
             reads=[rtmp[1]], writes=[rstd[1]])

    def rmsnorm_tile(self, src, t0, N, l, goff, out_fn, hc, sq, ps_stat, rtmp, rstd):
        S = self.S
        srcap = self.hbuf[src]
        b0, b1 = t0 // 128, (t0 + N) // 128
        for c in range(KC):
            hs, hr, hk = hc.next()
            S.op('sp', lambda e, hs=hs, c=c: e.dma_start(out=hs[:, :N], in_=srcap[c * 128:(c + 1) * 128, t0:t0 + N]),
                 reads=self.hres(src, b0, b1, [c]), writes=[hr], dma=hk)
            qs, qr, _ = sq.next()
            S.op('act', lambda e, hs=hs, qs=qs: e.activation(out=qs[:, :N], in_=hs[:, :N], func=AF.Square),
                 reads=[hr], writes=[qr])
            S.op('pe', lambda e, qs=qs, c=c: e.matmul(ps_stat[0][:, :N], lhsT=self.ones32, rhs=qs[:, :N],
                                                      start=(c == 0), stop=(c == KC - 1)),
                 reads=[qr, 'cf32'], writes=[ps_stat[1]])
        self.norm_rstd(ps_stat, N, rtmp, rstd, D, RMS_EPS)
        for c in range(KC):
            hs, hr, hk = hc.next()
            S.op('sp', lambda e, hs=hs, c=c: e.dma_start(out=hs[:, :N], in_=srcap[c * 128:(c + 1) * 128, t0:t0 + N]),
                 reads=self.hres(src, b0, b1, [c]), writes=[hr], dma=hk)
            oap, ores = out_fn(c)
            S.op('dve', lambda e, hs=hs, oap=oap, c=c: e.scalar_tensor_tensor(
                out=oap, in0=hs[:, :N], scalar=self.vcol(l, goff, c), in1=rstd[0][:, :N],
                op0=ALU.mult, op1=ALU.mult),
                reads=[hr, rstd[1], 'vecs'], writes=[ores])

    def ffn_phase(self, l, which, src, dst, brange, casts, to_out=False):
        nc, S = self.nc, self.S
        T = 512
        pre = 'ffn%d_' % which
        wg_d = getattr(self, 'wb_' + pre + 'w_gate')
        wu_d = getattr(self, 'wb_' + pre + 'w_up')
        wd_d = getattr(self, 'wb_' + pre + 'w_down')
        goff_pre = V_F1PRE if which == 1 else V_F2PRE
        goff_post = V_F1POST if which == 1 else V_F2POST
        dstap = self.yT if to_out else self.hbuf[dst]
        dst_toff = -self.NL * 128 if to_out else 0
        with ExitStack() as es:
            def sb(name, shape, dt):
                return es.enter_context(nc.sbuf_tensor('p%d_%s' % (S.phase, name), shape, dt))
            xn = sb('xn', [128, KC, T], BF16)
            hid = sb('hid', [128, FC, T], BF16)
            f1 = sb('f1', [128, KC, T], F32)
            wg = Rot([sb('wg%d' % i, [128, KC, 256], BF16) for i in range(2)], 'wg')
            wu = Rot([sb('wu%d' % i, [128, KC, 256], BF16) for i in range(2)], 'wu')
            wd = Rot([sb('wd%d' % i, [128, 11, 256], BF16) for i in range(2)], 'wd')
            hc = Rot([sb('hc%d' % i, [128, T], F32) for i in range(3)], 'hc')
            sq = Rot([sb('sq%d' % i, [128, T], F32) for i in range(2)], 'sq')
            sg = Rot([sb('sg%d' % i, [128, T], F32) for i in range(2)], 'sg')
            tm = Rot([sb('tm%d' % i, [128, T], F32) for i in range(2)], 'tm')
            st = Rot([sb('st%d' % i, [128, T], F32) for i in range(3)], 'st')
            rtmp = (sb('rtmp', [128, T], F32), 'rtmp')
            rstd = (sb('rstd', [128, T], F32), 'rstd')
            ps_stat = (self.ps[7], ('psx', 7))
            with nc.Block() as block:
                if not self.dbg.get('nocast'):
                    casts()

                def do_tile(tb0, tb1):
                    t0 = tb0 * 128
                    N = (tb1 - tb0) * 128
                    if self.dbg.get('ffn_level', 9) < 1:
                        return
                    self.rmsnorm_tile(src, t0, N, l, goff_pre,
                                      lambda c: (xn[:, c, :N], ('xn', c)), hc, sq, ps_stat, rtmp, rstd)
                    xn_res = [('xn', c) for c in range(KC)]
                    if self.dbg.get('ffn_level', 9) < 2:
                        return
                    for fg in range(22):
                        gs, gr, gk = wg.next()
                        us, ur, uk = wu.next()
                        S.op('sp', lambda e, gs=gs, fg=fg: e.dma_start(
                            out=gs[:], in_=wg_d[:, fg * 256:(fg + 1) * 256].rearrange("(kc p) f -> p kc f", p=128)),
                            reads=[('wb', pre + 'w_gate')], writes=[gr], dma=gk)
                        S.op('sp', lambda e, us=us, fg=fg: e.dma_start(
                            out=us[:], in_=wu_d[:, fg * 256:(fg + 1) * 256].rearrange("(kc p) f -> p kc f", p=128)),
                            reads=[('wb', pre + 'w_up')], writes=[ur], dma=uk)
                        for sub in range(2):
                            j = fg * 2 + sub
                            pg, pgr, _ = self.psrot.next()
                            pu, pur, _ = self.psrot.next()
                            for kc in range(KC):
                                S.op('pe', lambda e, pg=pg, gs=gs, kc=kc, sub=sub: e.matmul(
                                    pg[:, :N], lhsT=gs[:, kc, sub * 128:(sub + 1) * 128], rhs=xn[:, kc, :N],
                                    start=(kc == 0), stop=(kc == KC - 1)),
                                    reads=[gr] + xn_res, writes=[pgr])
                            for kc in range(KC):
                                S.op('pe', lambda e, pu=pu, us=us, kc=kc, sub=sub: e.matmul(
                                    pu[:, :N], lhsT=us[:, kc, sub * 128:(sub + 1) * 128], rhs=xn[:, kc, :N],
                                    start=(kc == 0), stop=(kc == KC - 1)),
                                    reads=[ur] + xn_res, writes=[pur])
                            ss, sr, _ = sg.next()
                            S.op('act', lambda e, ss=ss, pg=pg: e.activation(out=ss[:, :N], in_=pg[:, :N], func=AF.Silu),
                                 reads=[pgr], writes=[sr])
                            S.op('dve', lambda e, ss=ss, pu=pu, j=j: e.tensor_tensor(
                                out=hid[:, j, :N], in0=pu[:, :N], in1=ss[:, :N], op=ALU.mult),
                                reads=[sr, pur], writes=[('hid', j)])
                    if self.dbg.get('ffn_level', 9) < 3:
                        return
                    for mg in range(8):
                        pd = [self.psrot.next() for _ in range(2)]
                        for jg in range(4):
                            ws, wr, wk = wd.next()
                            S.op('sp', lambda e, ws=ws, jg=jg, mg=mg: e.dma_start(
                                out=ws[:], in_=wd_d[jg * 1408:(jg + 1) * 1408, mg * 256:(mg + 1) * 256]
                                .rearrange("(j p) m -> p j m", p=128)),
                                reads=[('wb', pre + 'w_down')], writes=[wr], dma=wk)
                            for mi in range(2):
                                for jj in range(11):
                                    j = jg * 11 + jj
                                    S.op('pe', lambda e, ws=ws, mi=mi, jj=jj, j=j, jg=jg, p=pd[mi][0]: e.matmul(
                                        p[:, :N], lhsT=ws[:, jj, mi * 128:(mi + 1) * 128], rhs=hid[:, j, :N],
                                        start=(j == 0), stop=(j == FC - 1)),
                                        reads=[wr, ('hid', j)], writes=[pd[mi][1]])
                        for mi in range(2):
                            m = mg * 2 + mi
                            p, pr, _ = pd[mi]
                            S.op('dve', lambda e, p=p, m=m: e.tensor_copy(out=f1[:, m, :N], in_=p[:, :N]),
                                 reads=[pr], writes=[('f1', m)])
                    for m in range(KC):
                        qs, qr, _ = sq.next()
                        S.op('act', lambda e, m=m, qs=qs: e.activation(out=qs[:, :N], in_=f1[:, m, :N], func=AF.Square),
                             reads=[('f1', m)], writes=[qr])
                        S.op('pe', lambda e, qs=qs, m=m: e.matmul(ps_stat[0][:, :N], lhsT=self.ones32, rhs=qs[:, :N],
                                                                  start=(m == 0), stop=(m == KC - 1)),
                             reads=[qr, 'cf32'], writes=[ps_stat[1]])
                    if self.dbg.get('ffn_level', 9) < 4:
                        return
                    self.norm_rstd(ps_stat, N, rtmp, rstd, D, RMS_EPS)
                    srcap = self.hbuf[src]
                    for c in range(KC):
                        hs, hr, hk = hc.next()
                        S.op('sp', lambda e, hs=hs, c=c: e.dma_start(out=hs[:, :N], in_=srcap[c * 128:(c + 1) * 128, t0:t0 + N]),
                             reads=self.hres(src, tb0, tb1, [c]), writes=[hr], dma=hk)
                        ts, tr, _ = tm.next()
                        S.op('dve', lambda e, ts=ts, c=c: e.scalar_tensor_tensor(
                            out=ts[:, :N], in0=f1[:, c, :N], scalar=self.vcol(l, goff_post, c), in1=rstd[0][:, :N],
                            op0=ALU.mult, op1=ALU.mult),
                            reads=[('f1', c), rstd[1], 'vecs'], writes=[tr])
                        os_, orr, ok = st.next()
                        S.op('dve', lambda e, ts=ts, hs=hs, os_=os_: e.scalar_tensor_tensor(
                            out=os_[:, :N], in0=ts[:, :N], scalar=0.5, in1=hs[:, :N], op0=ALU.mult, op1=ALU.add),
                            reads=[tr, hr], writes=[orr])
                        if to_out:
                            wres = [('out', c, b) for b in range(tb0, tb1)]
                        else:
                            wres = self.hres(dst, tb0, tb1, [c])
                        S.op('pool', lambda e, os_=os_, c=c: e.dma_start(
                            out=dstap[c * 128:(c + 1) * 128, t0 + dst_toff:t0 + dst_toff + N], in_=os_[:, :N]),
                            reads=[orr], writes=wres, dma=ok)

                for (tb0, tb1) in tiles_of(brange[0], brange[1], 4)[:self.dbg.get('ffn_tiles', 99)]:
                    do_tile(tb0, tb1)
                S.emit(block)

    def m1_phase(self, l, src, kr, mr, casts):
        nc, S = self.nc, self.S
        kb0, kb1 = kr
        mb0, mb1 = mr
        u0 = kb0 * 128
        NU = (kb1 - kb0) * 128
        T = 512
        win = self.wb_w_in
        with ExitStack() as es:
            def sb(name, shape, dt):
                return es.enter_context(nc.sbuf_tensor('p%d_%s' % (S.phase, name), shape, dt))
            u = sb('u', [128, KC, NU], BF16)
            cst = Rot([sb('cst%d' % i, [32, T], F32) for i in range(2)], 'cst')
            snt = Rot([sb('snt%d' % i, [32, T], F32) for i in range(2)], 'snt')
            ws = Rot([sb('ws%d' % i, [128, KC, 256], BF16) for i in range(3)], 'ws')
            hc = Rot([sb('hc%d' % i, [128, T], F32) for i in range(3)], 'hc')
            sq = Rot([sb('sq%d' % i, [128, T], F32) for i in range(2)], 'sq')
            rtmp = (sb('rtmp', [128, T], F32), 'rtmp')
            rstd = (sb('rstd', [128, T], F32), 'rstd')
            stb = Rot([sb('stb%d' % i, [128, T], BF16) for i in range(3)], 'stb')
            stf = Rot([sb('stf%d' % i, [128, T], F32) for i in range(3)], 'stf')
            x32 = Rot([sb('x32_%d' % i, [32, T], F32) for i in range(2)], 'x32')
            r1 = Rot([sb('r1_%d' % i, [32, T], F32) for i in range(2)], 'r1')
            r2 = Rot([sb('r2_%d' % i, [32, T], F32) for i in range(2)], 'r2')
            sgt = Rot([sb('sgt%d' % i, [128, T], F32) for i in range(2)], 'sgt')
            ps_stat = (self.ps[7], ('psx', 7))
            with nc.Block() as block:
                casts()
                for (tb0, tb1) in tiles_of(kb0, kb1, 4):
                    t0 = tb0 * 128
                    N = (tb1 - tb0) * 128
                    o0 = t0 - u0
                    self.rmsnorm_tile(src, t0, N, l, V_MPRE,
                                      lambda c, o0=o0, N=N, tb0=tb0: (u[:, c, o0:o0 + N], ('u', c, tb0)),
                                      hc, sq, ps_stat, rtmp, rstd)
                ktiles = tiles_of(kb0, kb1, 4)
                mtiles = tiles_of(mb0, mb1, 4)

                def ures(tb0):
                    for (a, b) in ktiles:
                        if a <= tb0 < b:
                            return [('u', c, a) for c in range(KC)]
                    raise AssertionError

                def ures_range(tb0, tb1):
                    r = []
                    for (a, b) in ktiles:
                        if a < tb1 and tb0 < b:
                            r += [('u', c, a) for c in range(KC)]
                    return r

                def load_w(cg):
                    s_, r_, k_ = ws.next()
                    S.op('sp', lambda e, s_=s_, cg=cg: e.dma_start(
                        out=s_[:], in_=win[:, cg * 256:(cg + 1) * 256].rearrange("(kc p) f -> p kc f", p=128)),
                        reads=[('wb', 'w_in')], writes=[r_], dma=k_)
                    return s_, r_

                def proj(s_, r_, sub, tb0, tb1):
                    N = (tb1 - tb0) * 128
                    o0 = tb0 * 128 - u0
                    p, pr, _ = self.psrot.next()
                    ur = ures_range(tb0, tb1)
                    for kc in range(KC):
                        S.op('pe', lambda e, p=p, s_=s_, kc=kc, sub=sub, o0=o0, N=N: e.matmul(
                            p[:, :N], lhsT=s_[:, kc, sub * 128:(sub + 1) * 128], rhs=u[:, kc, o0:o0 + N],
                            start=(kc == 0), stop=(kc == KC - 1)),
                            reads=[r_] + ur, writes=[pr])
                    return p, pr, N, o0

                def store(dram, row0, stslot, dt_rows, tb0, N, resname):
                    os_, orr, ok = stslot
                    S.op('pool', lambda e: e.dma_start(out=dram[row0:row0 + 128, tb0 * 128:tb0 * 128 + N], in_=os_[:, :N]),
                         reads=[orr], writes=[(resname, row0 // 128, b) for b in range(tb0, tb0 + N // 128)], dma=ok)

                for cg in range(48):
                    col0 = cg * 256
                    if 3072 <= col0 < 4096:
                        sa, ra = load_w(cg)
                        sgw, rg = load_w(cg + 4)
                        for sub in range(2):
                            ch = (col0 - 3072) // 128 + sub
                            for (tb0, tb1) in ktiles:
                                pa, par, N, o0 = proj(sa, ra, sub, tb0, tb1)
                                pg, pgr, _, _ = proj(sgw, rg, sub, tb0, tb1)
                                g_, gr_, _ = sgt.next()
                                S.op('act', lambda e, g_=g_, pg=pg, N=N: e.activation(out=g_[:, :N], in_=pg[:, :N], func=AF.Sigmoid),
                                     reads=[pgr], writes=[gr_])
                                slot = stb.next()
                                S.op('dve', lambda e, slot=slot, pa=pa, g_=g_, N=N: e.tensor_tensor(
                                    out=slot[0][:, :N], in0=pa[:, :N], in1=g_[:, :N], op=ALU.mult),
                                    reads=[par, gr_], writes=[slot[1]])
                                store(self.y_d, ch * 128, slot, None, tb0, N, 'y_d')
                        continue
                    if 4096 <= col0 < 5120:
                        continue
                    s_, r_ = load_w(cg)
                    if col0 < 2560:
                        isq = col0 < 2048
                        tl = mtiles if isq else ktiles
                        for sub in range(2):
                            head = (col0 // 128 + sub) if isq else ((col0 - 2048) // 128 + sub)
                            dram = self.q_d if isq else self.k_d
                            for (tb0, tb1) in tl:
                                p, pr, N, o0 = proj(s_, r_, sub, tb0, tb1)
                                cs, csr, csk = cst.next()
                                sn, snr, snk = snt.next()
                                S.op('sp', lambda e, cs=cs, tb0=tb0, N=N: e.dma_start(
                                    out=cs[:, :N], in_=self.cos_d[:, tb0 * 128:tb0 * 128 + N]), writes=[csr], dma=csk)
                                S.op('sp', lambda e, sn=sn, tb0=tb0, N=N: e.dma_start(
                                    out=sn[:, :N], in_=self.sin_d[:, tb0 * 128:tb0 * 128 + N]), writes=[snr], dma=snk)
                                xs, xr, _ = x32.next()
                                S.op('act', lambda e, xs=xs, p=p, N=N: e.activation(out=xs[:, :N], in_=p[0:32, :N], func=AF.Copy),
                                     reads=[pr], writes=[xr])
                                p2, p2r, _ = self.psrot.next()
                                S.op('pe', lambda e, p2=p2, xs=xs, N=N: e.matmul(
                                    p2[0:32, :N], lhsT=self.pswap, rhs=xs[:, :N], start=True, stop=True),
                                    reads=[xr, 'cf32'], writes=[p2r])
                                a1, a1r, _ = r1.next()
                                S.op('dve', lambda e, a1=a1, xs=xs, cs=cs, N=N: e.tensor_tensor(
                                    out=a1[:, :N], in0=xs[:, :N], in1=cs[:, :N], op=ALU.mult),
                                    reads=[xr, csr], writes=[a1r])
                                a2, a2r, _ = r2.next()
                                S.op('dve', lambda e, a2=a2, p2=p2, sn=sn, N=N: e.tensor_tensor(
                                    out=a2[:, :N], in0=p2[0:32, :N], in1=sn[:, :N], op=ALU.mult),
                                    reads=[p2r, snr], writes=[a2r])
                                slot = stb.next()
                                S.op('dve', lambda e, slot=slot, a1=a1, a2=a2, N=N: e.tensor_tensor(
                                    out=slot[0][0:32, :N], in0=a1[:, :N], in1=a2[:, :N], op=ALU.add),
                                    reads=[a1r, a2r], writes=[slot[1]])
                                S.op('act', lambda e, slot=slot, p=p, N=N: e.activation(
                                    out=slot[0][32:64, :N], in_=p[32:64, :N], func=AF.Copy),
                                    reads=[pr], writes=[slot[1]])
                                S.op('act', lambda e, slot=slot, p=p, N=N: e.activation(
                                    out=slot[0][64:128, :N], in_=p[64:128, :N], func=AF.Copy),
                                    reads=[pr], writes=[slot[1]])
                                store(dram, head * 128, slot, None, tb0, N, 'q_d' if isq else 'k_d')
                    elif col0 < 3072:
                        half = (col0 - 2560) // 256
                        for b in range(kb0, kb1):
                            o0 = b * 128 - u0
                            p, pr, _ = self.psrot.next()
                            ur = ures(b)
                            for kc in range(KC):
                                S.op('pe', lambda e, p=p, kc=kc, o0=o0, s_=s_: e.matmul(
                                    p[:, :256], lhsT=u[:, kc, o0:o0 + 128], rhs=s_[:, kc, :],
                                    start=(kc == 0), stop=(kc == KC - 1)),
                                    reads=[r_] + ur, writes=[pr])
                            slot = stb.next()
                            S.op('act', lambda e, slot=slot, p=p: e.activation(out=slot[0][:, :256], in_=p[:, :256], func=AF.Copy),
                                 reads=[pr], writes=[slot[1]])
                            S.op('pool', lambda e, slot=slot, b=b, half=half: e.dma_start(
                                out=self.v_d[b * 128:(b + 1) * 128, half * 256:(half + 1) * 256], in_=slot[0][:, :256]),
                                reads=[slot[1]], writes=[('v_d', b, half)], dma=slot[2])
                    elif col0 < 6144:
                        for sub in range(2):
                            ch = (col0 - 5120) // 128 + sub
                            for (tb0, tb1) in ktiles:
                                p, pr, N, o0 = proj(s_, r_, sub, tb0, tb1)
                                slot = stf.next()
                                S.op('act', lambda e, slot=slot, p=p, N=N: e.activation(out=slot[0][:, :N], in_=p[:, :N], func=AF.Copy),
                                     reads=[pr], writes=[slot[1]])
                                store(self.p_d, ch * 128, slot, None, tb0, N, 'p_d')
                    else:
                        for sub in range(2):
                            ch = (col0 - 6144) // 128 + sub
                            for (tb0, tb1) in mtiles:
                                p, pr, N, o0 = proj(s_, r_, sub, tb0, tb1)
                                slot = stf.next()
                                S.op('act', lambda e, slot=slot, p=p, N=N: e.activation(out=slot[0][:, :N], in_=p[:, :N], func=AF.Sigmoid),
                                     reads=[pr], writes=[slot[1]])
                                store(self.g_d, ch * 128, slot, None, tb0, N, 'g_d')
                S.emit(block)

    def m2_phase(self, l, src, dst, mr):
        nc, S = self.nc, self.S
        mb0, mb1 = mr
        T = 384
        with ExitStack() as es:
            def sb(name, shape, dt):
                return es.enter_context(nc.sbuf_tensor('p%d_%s' % (S.phase, name), shape, dt))
            qT = sb('qT', [128, 16, T], BF16)
            kT = sb('kT', [128, 4, T + 256], BF16)
            vv = sb('vv', [128, 5, 512], BF16)
            aT = sb('aT', [128, 16, T], BF16)
            mg = sb('mg', [128, 16, T], F32)
            ycv = sb('ycv', [128, 8, T + 32], BF16)
            pin = sb('pin', [128, 8, T + 16], F32)
            cacc = sb('cacc', [128, 8, T], F32)
            cn = sb('cn', [128, 8, T], BF16)
            pyl = sb('pyl', [128, 8, T], BF16)
            ws = Rot([sb('ws%d' % i, [128, KC, 256], BF16) for i in range(2)], 'ws')
            gt = Rot([sb('gt%d' % i, [128, T], F32) for i in range(3)], 'gt')
            tq = Rot([sb('tq%d' % i, [128, T], F32) for i in range(2)], 'tq')
            esb = Rot([sb('esb%d' % i, [128, 384], F32) for i in range(2)], 'esb')
            pn = Rot([sb('pn%d' % i, [128, 384], BF16) for i in range(2)], 'pn')
            scb = Rot([sb('scb%d' % i, [128, 384], F32) for i in range(2)], 'scb')
            pT = Rot([sb('pT%d' % i, [128, 384], BF16) for i in range(2)], 'pT')
            sm = Rot([sb('sm%d' % i, [128, 8], F32) for i in range(3)], 'sm')
            dg = Rot([sb('dg%d' % i, [128, 128], BF16) for i in range(4)], 'dg')
            ta = sb('ta', [128, T + 16], F32)
            tb_ = sb('tb', [128, T + 16], F32)
            rcb = Rot([sb('rcb%d' % i, [128, T], F32) for i in range(2)], 'rcb')
            vld = sb('vld', [128, T], F32)
            kbt = sb('kbt', [128, T + 256], BF16)
            mk = sb('mk', [128, 3, 384], BF16)
            mean = sb('mean', [128, T], F32)
            msq = sb('msq', [128, T], F32)
            rtmp = (sb('rtmp', [128, T], F32), 'rtmp')
            rstd = (sb('rstd', [128, T], F32), 'rstd')
            hc = Rot([sb('hc%d' % i, [128, T], F32) for i in range(3)], 'hc')
            st = Rot([sb('st%d' % i, [128, T], F32) for i in range(3)], 'st')
            ps6 = (self.ps[6], ('psx', 6))
            ps7 = (self.ps[7], ('psx', 7))
            srcap = self.hbuf[src]
            dstap = self.hbuf[dst]
            with nc.Block() as block:
                def do_tile(tb0, tb1):
                    nb = tb1 - tb0
                    t0 = tb0 * 128
                    N = nb * 128
                    NK = N + 256
                    S.op('sp', lambda e, t0=t0, N=N: e.dma_start(
                        out=qT[:, :, :N], in_=self.q_d[:, t0:t0 + N].rearrange("(h p) t -> p h t", p=128)),
                        reads=[('q_d', h, b) for h in range(16) for b in range(tb0, tb1)],
                        writes=[('qT', h) for h in range(16)], dma='ld_q')
                    S.op('sp', lambda e, t0=t0, NK=NK: e.dma_start(
                        out=kT[:, :, :NK], in_=self.k_d[:, t0 - 128:t0 - 128 + NK].rearrange("(h p) t -> p h t", p=128)),
                        reads=[('k_d', h, b) for h in range(4) for b in range(tb0 - 1, tb1 + 1)],
                        writes=['kT'], dma='ld_k')
                    S.op('sp', lambda e, t0=t0, nb=nb: e.dma_start(
                        out=vv[:, :nb + 2, :], in_=self.v_d[t0 - 128:t0 + (nb + 1) * 128, :].rearrange("(b p) f -> p b f", p=128)),
                        reads=[('v_d', b, hf) for b in range(tb0 - 1, tb1 + 1) for hf in range(2)],
                        writes=['vv'], dma='ld_v')
                    S.op('sp', lambda e, t0=t0, NK=NK: e.dma_start(out=kbt[:, :NK], in_=self.kb_d[0:1, t0 - 128:t0 - 128 + NK].partition_broadcast(128)),
                         writes=['kbt'], dma='ld_kb')
                    for b in range(nb):
                        S.op('dve', lambda e, b=b: e.tensor_tensor(out=mk[:, b, :], in0=self.winmask, in1=kbt[:, b * 128:b * 128 + 384], op=ALU.add),
                             reads=['cbf', 'kbt'], writes=[('mk', b)])
                    S.op('sp', lambda e, t0=t0, N=N: e.dma_start(out=vld[:, :N], in_=self.valid_d[0:1, t0:t0 + N].partition_broadcast(128)),
                         writes=['vld'], dma='ld_vl')
                    S.op('sp', lambda e, t0=t0, N=N: e.dma_start(
                        out=ycv[:, :, :N + 32], in_=self.y_d[:, t0 - 16:t0 + N + 16].rearrange("(c p) t -> p c t", p=128)),
                        reads=[('y_d', c, b) for c in range(8) for b in range(tb0 - 1, tb1 + 1)],
                        writes=['ycv'], dma='ld_y')
                    S.op('sp', lambda e, t0=t0, N=N: e.dma_start(
                        out=pin[:, :, :N + 16], in_=self.p_d[:, t0 - 8:t0 + N + 8].rearrange("(c p) t -> p c t", p=128)),
                        reads=[('p_d', c, b) for c in range(8) for b in range(tb0 - 1, tb1 + 1)],
                        writes=['pin'], dma='ld_p')
                    BR = self.dbg.get('branches', [0, 1, 2])
                    single = len(BR) == 1
                    for h in range(16):
                        kvh = h // 4
                        sink_col = self.vcol(l, V_SINK, h)
                        for b in range(nb):
                            p, pr, _ = self.psrot.next()
                            S.op('pe', lambda e, p=p, h=h, b=b, kvh=kvh: e.matmul(
                                p[:, :384], lhsT=qT[:, h, b * 128:(b + 1) * 128], rhs=kT[:, kvh, b * 128:b * 128 + 384],
                                start=True, stop=True), reads=[('qT', h), 'kT'], writes=[pr])
                            c_, cr_, _ = scb.next()
                            S.op('dve', lambda e, p=p, b=b, c_=c_: e.tensor_tensor(out=c_[:, :], in0=p[:, :384], in1=mk[:, b, :], op=ALU.add),
                                 reads=[pr, ('mk', b)], writes=[cr_])
                            p, pr = c_, cr_
                            s_, sr_, _ = sm.next()
                            S.op('dve', lambda e, s_=s_, p=p: e.tensor_reduce(out=s_[:, 0:1], in_=p[:, :384], axis=AX.X, op=ALU.max),
                                 reads=[pr], writes=[sr_])
                            S.op('dve', lambda e, s_=s_, sink_col=sink_col: e.tensor_scalar(
                                out=s_[:, 1:2], in0=s_[:, 0:1], scalar1=float(SCALE), scalar2=sink_col, op0=ALU.mult, op1=ALU.max),
                                reads=[sr_, 'vecs'], writes=[sr_], sync_same=True)
                            S.op('dve', lambda e, s_=s_: e.tensor_scalar(
                                out=s_[:, 2:3], in0=s_[:, 1:2], scalar1=-1.0, scalar2=None, op0=ALU.mult),
                                reads=[sr_], writes=[sr_], sync_same=True)
                            e_, er_, _ = esb.next()
                            S.op('act', lambda e, e_=e_, p=p, s_=s_: e.activation(
                                out=e_[:, :], in_=p[:, :384], func=AF.Exp, bias=s_[:, 2:3], scale=float(SCALE)),
                                reads=[pr, sr_], writes=[er_])
                            S.op('dve', lambda e, e_=e_, s_=s_: e.tensor_reduce(out=s_[:, 3:4], in_=e_[:, :], axis=AX.X, op=ALU.add),
                                 reads=[er_, sr_], writes=[sr_])
                            S.op('act', lambda e, s_=s_, sink_col=sink_col: e.activation(
                                out=s_[:, 4:5], in_=s_[:, 1:2], func=AF.Exp, bias=sink_col, scale=-1.0),
                                reads=[sr_, 'vecs'], writes=[sr_])
                            S.op('dve', lambda e, s_=s_: e.tensor_tensor(out=s_[:, 5:6], in0=s_[:, 3:4], in1=s_[:, 4:5], op=ALU.add),
                                 reads=[sr_], writes=[sr_], sync_same=True)
                            S.op('dve', lambda e, s_=s_: e.reciprocal(out=s_[:, 6:7], in_=s_[:, 5:6]), reads=[sr_], writes=[sr_],
                                 sync_same=True)
                            n_, nr_, _ = pn.next()
                            S.op('act', lambda e, n_=n_, e_=e_, s_=s_: e.activation(
                                out=n_[:, :], in_=e_[:, :], func=AF.Copy, scale=s_[:, 6:7]),
                                reads=[er_, sr_], writes=[nr_])
                            if self.dbg.get('dump_att') and tb0 == 4 and h == 0 and b == 0:
                                S.op('sp', lambda e, n_=n_: e.dma_start(out=self.dp[:, :], in_=n_[:, :]), reads=[nr_], writes=['dbgp'], dma='fin')
                                S.op('sp', lambda e, e_=e_: e.dma_start(out=self.de[:, :], in_=e_[:, :]), reads=[er_], writes=['dbge'], dma='fin')
                                S.op('sp', lambda e, s_=s_: e.dma_start(out=self.ds[:, :], in_=s_[:, :]), reads=[sr_, nr_], writes=['dbgs'], dma='fin')
                            p2r = 'pst'
                            p2b = self.pst
                            for kb in range(3):
                                S.op('pe', lambda e, p2b=p2b, n_=n_, kb=kb: e.transpose(
                                    p2b[:, kb * 128:(kb + 1) * 128], n_[:, kb * 128:(kb + 1) * 128], self.ident),
                                    reads=[nr_, 'cbf'], writes=[p2r])
                            t_, tr_, _ = pT.next()
                            S.op('act', lambda e, t_=t_, p2b=p2b: e.activation(out=t_[:, :], in_=p2b[:, 0:384], func=AF.Copy),
                                 reads=[p2r], writes=[tr_])
                            p3, p3r, _ = self.psrot.next()
                            for kb in range(3):
                                S.op('pe', lambda e, p3=p3, t_=t_, kb=kb, b=b, kvh=kvh: e.matmul(
                                    p3[:, :128], lhsT=vv[:, b + kb, kvh * 128:(kvh + 1) * 128], rhs=t_[:, kb * 128:(kb + 1) * 128],
                                    start=(kb == 0), stop=(kb == 2)), reads=[tr_, 'vv'], writes=[p3r])
                            S.op('act', lambda e, p3=p3, h=h, b=b: e.activation(
                                out=aT[:, h, b * 128:(b + 1) * 128], in_=p3[:, :128], func=AF.Copy),
                                reads=[p3r], writes=[('aT', h)])

                    if self.dbg.get('dump2') and tb0 == 4:
                        for h in range(16):
                            S.op('pool', lambda e, h=h: e.dma_start(out=self.d2a[:, h * 384:(h + 1) * 384], in_=aT[:, h, :]),
                                 reads=[('aT', h)], writes=[('dbga', h)], dma='fin')
                            S.op('pool', lambda e, h=h: e.dma_start(out=self.d2q[:, h * 384:(h + 1) * 384], in_=qT[:, h, :]),
                                 reads=[('qT', h)], writes=[('dbgq', h)], dma='fin')
                        for kv in range(4):
                            S.op('pool', lambda e, kv=kv: e.dma_start(out=self.d2k[:, kv * 640:(kv + 1) * 640], in_=kT[:, kv, :]),
                                 reads=['kT'], writes=[('dbgk', kv)], dma='fin')
                        for bb in range(5):
                            S.op('pool', lambda e, bb=bb: e.dma_start(out=self.d2v[:, bb * 512:(bb + 1) * 512], in_=vv[:, bb, :]),
                                 reads=['vv'], writes=[('dbgv', bb)], dma='fin')
                    if self.dbg.get('dump_att') and tb0 in (1, 4):
                        ti = 0 if tb0 == 1 else 1
                        S.op('sp', lambda e, ti=ti: e.dma_start(out=self.da[ti].rearrange("p (h t) -> p h t", h=16), in_=aT[:, :, :]),
                             reads=[('aT', h) for h in range(16)], writes=[('dbga', ti)], dma='fin')
                    def branch_proj(wname, nk, rhs_fn, rhs_res, bi, mode=None, N=N, t0=t0, tb0=tb0, tb1=tb1):
                        if mode is None:
                            mode = {0: 'set', 1: 'add', 2: 'final'}.get(bi)
                        wdram = getattr(self, 'wb_' + wname)
                        for cg in range(8):
                            s_, r_, k_ = ws.next()
                            S.op('sp', lambda e, s_=s_, cg=cg: e.dma_start(
                                out=s_[:, :nk, :], in_=wdram[:, cg * 256:(cg + 1) * 256].rearrange("(kc p) f -> p kc f", p=128)),
                                reads=[('wb', wname)], writes=[r_], dma=k_)
                            for sub in range(2):
                                m = cg * 2 + sub
                                p, pr, _ = self.psrot.next()
                                for kc in range(nk):
                                    S.op('pe', lambda e, p=p, s_=s_, kc=kc, sub=sub: e.matmul(
                                        p[:, :N], lhsT=s_[:, kc, sub * 128:(sub + 1) * 128], rhs=rhs_fn(kc),
                                        start=(kc == 0), stop=(kc == nk - 1)), reads=[r_] + rhs_res, writes=[pr])
                                if bi is None:
                                    S.op('dve', lambda e, p=p, m=m: e.tensor_copy(out=mg[:, m, :N], in_=p[:, :N]),
                                         reads=[pr], writes=[('mg', m)])
                                    continue
                                g_, gr_, gk_ = gt.next()
                                row0 = bi * D + m * 128
                                S.op('sp', lambda e, g_=g_, row0=row0: e.dma_start(out=g_[:, :N], in_=self.g_d[row0:row0 + 128, t0:t0 + N]),
                                     reads=[('g_d', row0 // 128, b) for b in range(tb0, tb1)], writes=[gr_], dma=gk_)
                                if mode == 'set':
                                    S.op('dve', lambda e, p=p, g_=g_, m=m: e.tensor_tensor(
                                        out=mg[:, m, :N], in0=p[:, :N], in1=g_[:, :N], op=ALU.mult),
                                        reads=[pr, gr_], writes=[('mg', m)])
                                else:
                                    q_, qr_, _ = tq.next()
                                    S.op('dve', lambda e, p=p, g_=g_, q_=q_: e.tensor_tensor(
                                        out=q_[:, :N], in0=p[:, :N], in1=g_[:, :N], op=ALU.mult),
                                        reads=[pr, gr_], writes=[qr_])
                                    if mode == 'add':
                                        S.op('dve', lambda e, q_=q_, m=m: e.tensor_tensor(
                                            out=mg[:, m, :N], in0=mg[:, m, :N], in1=q_[:, :N], op=ALU.add),
                                            reads=[qr_, ('mg', m)], writes=[('mg', m)])
                                    else:
                                        S.op('dve', lambda e, q_=q_, m=m: e.tensor_tensor(
                                            out=qT[:, m, :N], in0=mg[:, m, :N], in1=q_[:, :N], op=ALU.add),
                                            reads=[qr_, ('mg', m)], writes=[('qT', m)])

                    if 0 in BR:
                        branch_proj('w_attn_proj', 16, lambda kc: aT[:, kc, :N], [('aT', h) for h in range(16)], 0, 'set')

                    for c in range(8):
                        p, pr, _ = self.psrot.next()
                        for j in range(31):
                            d_, dr_, _ = dg.next()
                            S.op('dve', lambda e, d_=d_, j=j, c=c: e.tensor_scalar(
                                out=d_[:, :], in0=self.ident, scalar1=self.vcol(l, V_CDW, j * 8 + c), scalar2=None, op0=ALU.mult),
                                reads=['cbf', 'vecs'], writes=[dr_])
                            S.op('pe', lambda e, p=p, d_=d_, j=j, c=c: e.matmul(
                                p[:, :N], lhsT=d_[:, :], rhs=ycv[:, c, j + 1:j + 1 + N], start=(j == 0), stop=(j == 30)),
                                reads=[dr_, 'ycv'], writes=[pr])
                        S.op('dve', lambda e, p=p, c=c: e.tensor_scalar(
                            out=cacc[:, c, :N], in0=p[:, :N], scalar1=self.vcol(l, V_CB, c), scalar2=None, op0=ALU.add),
                            reads=[pr, 'vecs'], writes=[('cacc', c)])
                    for c in range(8):
                        q_, qr_, _ = tq.next()
                        S.op('act', lambda e, q_=q_, c=c: e.activation(out=q_[:, :N], in_=cacc[:, c, :N], func=AF.Square),
                             reads=[('cacc', c)], writes=[qr_])
                        S.op('pe', lambda e, c=c: e.matmul(ps6[0][:, :N], lhsT=self.ones32, rhs=cacc[:, c, :N],
                                                           start=(c == 0), stop=(c == 7)),
                             reads=[('cacc', c), 'cf32'], writes=[ps6[1]])
                        S.op('pe', lambda e, q_=q_, c=c: e.matmul(ps7[0][:, :N], lhsT=self.ones32, rhs=q_[:, :N],
                                                                  start=(c == 0), stop=(c == 7)),
                             reads=[qr_, 'cf32'], writes=[ps7[1]])
                    S.op('dve', lambda e: e.tensor_scalar(out=mean[:, :N], in0=ps6[0][:, :N], scalar1=1.0 / 1024, scalar2=None, op0=ALU.mult),
                         reads=[ps6[1]], writes=['mean'])
                    S.op('dve', lambda e: e.tensor_tensor(out=msq[:, :N], in0=mean[:, :N], in1=mean[:, :N], op=ALU.mult),
                         reads=['mean'], writes=['msq'])
                    S.op('dve', lambda e: e.scalar_tensor_tensor(out=msq[:, :N], in0=ps7[0][:, :N], scalar=1.0 / 1024, in1=msq[:, :N],
                                                                 op0=ALU.mult, op1=ALU.subtract),
                         reads=[ps7[1], 'msq'], writes=['msq'])
                    S.op('act', lambda e: e.activation(out=rtmp[0][:, :N], in_=msq[:, :N], func=AF.Sqrt, bias=float(LN_EPS), scale=1.0),
                         reads=['msq'], writes=[rtmp[1]])
                    S.op('dve', lambda e: e.reciprocal(out=rstd[0][:, :N], in_=rtmp[0][:, :N]), reads=[rtmp[1]], writes=[rstd[1]])
                    for c in range(8):
                        q_, qr_, _ = tq.next()
                        S.op('dve', lambda e, q_=q_, c=c: e.tensor_tensor(out=q_[:, :N], in0=cacc[:, c, :N], in1=mean[:, :N], op=ALU.subtract),
                             reads=[('cacc', c), 'mean'], writes=[qr_])
                        S.op('dve', lambda e, q_=q_: e.tensor_tensor(out=q_[:, :N], in0=q_[:, :N], in1=rstd[0][:, :N], op=ALU.mult),
                             reads=[qr_, rstd[1]], writes=[qr_])
                        S.op('act', lambda e, q_=q_, c=c: e.activation(
                            out=cn[:, c, :N], in_=q_[:, :N], func=AF.Silu, bias=self.vcol(l, V_LNB, c), scale=self.vcol(l, V_LNG, c)),
                            reads=[qr_, 'vecs'], writes=[('cn', c)])
                    if 1 in BR:
                        branch_proj('w_conv_proj', 8, lambda kc: cn[:, kc, :N], [('cn', c) for c in range(8)], 1, 'set' if single else 'add')

                    M = N + 16
                    pws, pwr, pwk = ws.next()
                    S.op('sp', lambda e, pws=pws: e.dma_start(
                        out=pws[:, :8, :], in_=self.wb_pool_w[:, :].rearrange("(gc p) f -> p gc f", p=128)),
                        reads=[('wb', 'pool_w')], writes=[pwr], dma=pwk)
                    for gi in range(4):
                        r_, rr_, rk_ = rcb.next()
                        S.op('sp', lambda e, r_=r_, gi=gi: e.dma_start(
                            out=r_[:, :N], in_=self.rc_d[gi:gi + 1, t0:t0 + N].partition_broadcast(128)),
                            writes=[rr_], dma=rk_)
                        for cc in range(2):
                            ch = gi * 2 + cc
                            X = pin[:, ch, :]
                            S.op('dve', lambda e, X=X: e.tensor_tensor(out=ta[:, 1:M], in0=X[:, 0:M - 1], in1=X[:, 1:M], op=ALU.add),
                                 reads=['pin'], writes=['ta'])
                            wsrc, wres, off = ta, 'ta', 0
                            if gi >= 1:
                                S.op('dve', lambda e: e.tensor_tensor(out=tb_[:, 2:M - 1], in0=ta[:, 1:M - 2], in1=ta[:, 3:M], op=ALU.add),
                                     reads=['ta'], writes=['tb'])
                                wsrc, wres = tb_, 'tb'
                            if gi >= 2:
                                S.op('dve', lambda e: e.tensor_tensor(out=ta[:, 4:M - 3], in0=tb_[:, 2:M - 5], in1=tb_[:, 6:M - 1], op=ALU.add),
                                     reads=['tb', 'ta'], writes=['ta'])
                                wsrc, wres = ta, 'ta'
                            if gi >= 3:
                                S.op('dve', lambda e: e.tensor_tensor(out=tb_[:, 8:M - 7], in0=ta[:, 4:M - 11], in1=ta[:, 12:M - 3], op=ALU.add),
                                     reads=['ta', 'tb'], writes=['tb'])
                                wsrc, wres = tb_, 'tb'
                            q_, qr_, _ = tq.next()
                            S.op('dve', lambda e, q_=q_, wsrc=wsrc, r_=r_: e.tensor_tensor(
                                out=q_[:, :N], in0=wsrc[:, 8:8 + N], in1=r_[:, :N], op=ALU.mult),
                                reads=[wres, rr_], writes=[qr_])
                            S.op('dve', lambda e, q_=q_, X=X, ch=ch: e.tensor_tensor(
                                out=cn[:, ch, :N], in0=q_[:, :N], in1=X[:, 8:8 + N], op=ALU.subtract),
                                reads=[qr_, 'pin'], writes=[('cn', ch)])
                        for ee in range(2):
                            p, pr, _ = self.psrot.next()
                            for cc in range(2):
                                S.op('pe', lambda e, p=p, gi=gi, cc=cc, ee=ee: e.matmul(
                                    p[:, :N], lhsT=pws[:, gi * 2 + cc, ee * 128:(ee + 1) * 128], rhs=cn[:, gi * 2 + cc, :N],
                                    start=(cc == 0), stop=(cc == 1)),
                                    reads=[pwr, ('cn', gi * 2), ('cn', gi * 2 + 1)], writes=[pr])
                            S.op('dve', lambda e, p=p, gi=gi, ee=ee: e.tensor_scalar(
                                out=pyl[:, gi * 2 + ee, :N], in0=p[:, :N], scalar1=self.vcol(l, V_PSC, gi * 2 + ee), scalar2=None, op0=ALU.mult),
                                reads=[pr, 'vecs'], writes=[('pyl', gi * 2 + ee)])
                    if 2 in BR:
                        branch_proj('w_pool_proj', 8, lambda kc: pyl[:, kc, :N], [('pyl', c) for c in range(8)], 2, 'set' if single else 'final')
                    if single:
                        for m in range(16):
                            S.op('dve', lambda e, m=m: e.tensor_copy(out=qT[:, m, :N], in_=mg[:, m, :N]),
                                 reads=[('mg', m)], writes=[('qT', m)])

                    branch_proj('w_out', 16, lambda kc: qT[:, kc, :N], [('qT', m) for m in range(16)], None)
                    for m in range(16):
                        q_, qr_, _ = tq.next()
                        S.op('act', lambda e, m=m, q_=q_: e.activation(out=q_[:, :N], in_=mg[:, m, :N], func=AF.Square),
                             reads=[('mg', m)], writes=[qr_])
                        S.op('pe', lambda e, q_=q_, m=m: e.matmul(ps7[0][:, :N], lhsT=self.ones32, rhs=q_[:, :N],
                                                                  start=(m == 0), stop=(m == 15)),
                             reads=[qr_, 'cf32'], writes=[ps7[1]])
                    self.norm_rstd(ps7, N, rtmp, rstd, D, RMS_EPS)
                    S.op('dve', lambda e: e.tensor_tensor(out=rstd[0][:, :N], in0=rstd[0][:, :N], in1=vld[:, :N], op=ALU.mult),
                         reads=[rstd[1], 'vld'], writes=[rstd[1]])
                    for c in range(KC):
                        hs, hr, hk = hc.next()
                        S.op('sp', lambda e, hs=hs, c=c: e.dma_start(out=hs[:, :N], in_=srcap[c * 128:(c + 1) * 128, t0:t0 + N]),
                             reads=self.hres(src, tb0, tb1, [c]), writes=[hr], dma=hk)
                        q_, qr_, _ = tq.next()
                        S.op('dve', lambda e, q_=q_, c=c: e.scalar_tensor_tensor(
                            out=q_[:, :N], in0=mg[:, c, :N], scalar=self.vcol(l, V_MPOST, c), in1=rstd[0][:, :N],
                            op0=ALU.mult, op1=ALU.mult), reads=[('mg', c), rstd[1], 'vecs'], writes=[qr_])
                        os_, orr, ok = st.next()
                        S.op('dve', lambda e, os_=os_, q_=q_, hs=hs: e.tensor_tensor(
                            out=os_[:, :N], in0=q_[:, :N], in1=hs[:, :N], op=ALU.add), reads=[qr_, hr], writes=[orr])
                        S.op('pool', lambda e, os_=os_, c=c: e.dma_start(
                            out=dstap[c * 128:(c + 1) * 128, t0:t0 + N], in_=os_[:, :N]),
                            reads=[orr], writes=self.hres(dst, tb0, tb1, [c]), dma=ok)

                for (tb0, tb1) in tiles_of(mb0, mb1, 3):
                    do_tile(tb0, tb1)
                S.emit(block)


_PROG_CACHE = {}


DBG = None


def get_prog(NL, stop_after=None):
    key = (NL, stop_after)
    if key not in _PROG_CACHE:
        _PROG_CACHE[key] = Prog(NL, stop_after, DBG)
    return _PROG_CACHE[key]


def _chunkcols(v):
    v = np.asarray(v, np.float32)
    return np.ascontiguousarray(v.reshape(-1, 128).T)


def build_vecs(inp, layers):
    cols = []
    for l in layers:
        parts = [inp['ffn1_pre_g'][l], inp['ffn1_post_g'][l], inp['mix_pre_g'][l], inp['mix_post_g'][l],
                 inp['ffn2_pre_g'][l], inp['ffn2_post_g'][l], inp['conv_dw_b'][l], inp['conv_ln_g'][l],
                 inp['conv_ln_b'][l], inp['pool_scale'][l]]
        blk = [_chunkcols(p) for p in parts]
        cdw = np.asarray(inp['conv_dw'][l], np.float32)
        blk.append(np.ascontiguousarray(cdw.reshape(31, 8, 128).transpose(2, 0, 1).reshape(128, 248)))
        blk.append(np.ascontiguousarray(np.broadcast_to(np.asarray(inp['sink'][l], np.float32)[None, :], (128, 16))))
        b = np.concatenate(blk, axis=1)
        assert b.shape == (128, VL), b.shape
        cols.append(b)
    return np.ascontiguousarray(np.concatenate(cols, axis=1))


def const_tables():
    cf = np.zeros((128, 160), np.float32)
    cf[:, 0:128] = 1.0
    for m in range(32):
        k = (m + 16) % 32
        cf[k, 128 + m] = 1.0
    cb = np.zeros((128, 640), np.float32)
    cb[:, 0:128] = np.eye(128, dtype=np.float32)
    qi = np.arange(128)[:, None]
    kj = np.arange(384)[None, :]
    inwin = np.abs(kj - 128 - qi) <= 128
    cb[:, 128:512] = np.where(inwin, 0.0, NEG)
    cb[0, 512:640] = 1.0
    return cf, cb.astype(ml_dtypes.bfloat16)


def core_tables(core, NL):
    halo = 128 * NL
    NT = OWN + 2 * halo
    q = core % CPS
    pos = q * OWN - halo + np.arange(NT)
    inseq = (pos >= 0) & (pos < SEQ)
    inv_freq = ROPE_THETA ** (-np.arange(0, 32, 2, dtype=np.float32) / np.float32(32))
    ang = pos.astype(np.float32)[None, :] * inv_freq.astype(np.float32)[:, None]
    cos = np.cos(ang).astype(np.float32)
    sin = np.sin(ang).astype(np.float32)
    cosT = np.concatenate([cos, cos], axis=0)
    sinT = np.concatenate([-sin, sin], axis=0)
    valid = inseq.astype(np.float32)[None, :]
    rc = np.zeros((4, NT), np.float32)
    for gi, s in enumerate((2, 4, 8, 16)):
        half = s // 2
        lo = np.clip(pos - half, 0, SEQ - 1)
        hi = np.clip(pos + half - 1, 0, SEQ - 1)
        cnt = np.maximum(hi - lo + 1, 1)
        rc[gi] = 1.0 / cnt.astype(np.float32)
    kb = np.where(inseq, 0.0, NEG).astype(np.float32)[None, :].astype(ml_dtypes.bfloat16)
    return dict(cosT=np.ascontiguousarray(cosT), sinT=np.ascontiguousarray(sinT), valid=valid, rc=rc, kbias=kb)


def core_xT(h, core, NL):
    halo = 128 * NL
    b, q = core // CPS, core % CPS
    s0 = q * OWN - halo
    s1 = q * OWN + OWN + halo
    xt = np.zeros((D, OWN + 2 * halo), np.float32)
    a0, a1 = max(s0, 0), min(s1, SEQ)
    xt[:, a0 - s0:a1 - s0] = h[b, a0:a1, :].T
    return xt


W_NAMES = ['ffn1_w_gate', 'ffn1_w_up', 'ffn1_w_down', 'w_in', 'w_attn_proj', 'w_conv_proj', 'pool_w',
           'w_pool_proj', 'w_out', 'ffn2_w_gate', 'ffn2_w_up', 'ffn2_w_down']


def run_layers(inp, h, layers, stop_after=None, cores=None):
    NL = len(layers)
    prog = get_prog(NL, stop_after)
    cf, cb = const_tables()
    vecs = build_vecs(inp, layers)
    wmap = {}
    for nm in W_NAMES:
        w = np.asarray(inp[nm])
        if nm == 'pool_w':
            w = w.reshape(w.shape[0], 1024, 256)
        wmap[nm] = np.ascontiguousarray(w[layers[0]:layers[-1] + 1])
    in_maps = []
    cores = list(range(NCORES)) if cores is None else cores
    for core in cores:
        m = dict(wmap)
        m['xT'] = core_xT(h, core, NL)
        m['vecs'] = vecs
        m['cf32'] = cf
        m['cbf'] = cb
        m.update(core_tables(core, NL))
        in_maps.append(m)
    res = run_bass_kernel_spmd(prog.nc, in_maps, core_ids=list(range(len(cores))))
    out = np.zeros((BATCH, SEQ, D), np.float32)
    for i, core in enumerate(cores):
        b, q = core // CPS, core % CPS
        out[b, q * OWN:(q + 1) * OWN, :] = np.asarray(res.results[i]['yT']).T
    return out


FUSED = False


def kernel(**inputs):
    inp = {k: np.asarray(v) for k, v in inputs.items()}
    h = np.asarray(inp['x'], np.float32)
    if FUSED:
        return run_layers(inp, h, list(range(DEPTH)))
    for l in range(DEPTH):
        h = run_layers(inp, h, [l])
    return h
```
